# Optimizing a Trainium2 kernel written in Bass

```python
import math
import jax
import jax.numpy as jnp
from jax import lax
import numpy as np


D_MODEL = 4096
BATCH = 2
SEQ = 4096
DEPTH = 2

ROPE_THETA = 500000.0
NORM_EPS = 1e-6
Q_BLOCK = 128
ROPE_FRACTION = 4

MLA_HEADS = 16
MLA_NOPE = 128
MLA_ROPE = 64
MLA_V = 128
MLA_Q_RANK = 1024
MLA_KV_RANK = 512

DIFF_HEADS = 8
DIFF_DIM = 128

DSA_HEADS = 16
DSA_DIM = 128
IDX_HEADS = 16
IDX_DIM = 64
TOPK_MAX = 256

FOX_HEADS = 16
FOX_DIM = 128

MLA_WIDTH = MLA_HEADS * MLA_V
DIFF_WIDTH = DIFF_HEADS * 2 * DIFF_DIM
DSA_WIDTH = DSA_HEADS * DSA_DIM
FOX_WIDTH = FOX_HEADS * FOX_DIM

EVEN_COLS = (MLA_Q_RANK, MLA_KV_RANK, MLA_ROPE, MLA_WIDTH, DIFF_WIDTH, DIFF_WIDTH, DIFF_WIDTH, DIFF_WIDTH)
ODD_COLS = (DSA_WIDTH, DSA_DIM, DSA_DIM, IDX_HEADS * IDX_DIM, IDX_DIM, IDX_HEADS, DSA_WIDTH, FOX_WIDTH, FOX_WIDTH, FOX_WIDTH, FOX_HEADS, FOX_WIDTH)

N_EVEN = (DEPTH + 1) // 2
N_ODD = DEPTH // 2

kernel_name = 'hybrid_mla_diff_dsa_fox_trunk'


def rms_norm(x, g):
    xf = x.astype(jnp.float32)
    y = xf * lax.rsqrt(jnp.mean(xf * xf, axis=-1, keepdims=True) + NORM_EPS)
    return (y * g.astype(jnp.float32)).astype(x.dtype)


def split_cols(a, sizes):
    pts = []
    acc = 0
    for s in sizes[:-1]:
        acc += s
        pts.append(acc)
    return jnp.split(a, pts, axis=-1)


def rope_tables(seq, rot_dim):
    inv = ROPE_THETA ** (-jnp.arange(0, rot_dim, 2, dtype=jnp.float32) / rot_dim)
    ang = jnp.arange(seq, dtype=jnp.float32)[:, None] * inv[None, :]
    return jnp.cos(ang), jnp.sin(ang)


def apply_rope(x, cos, sin):
    half = cos.shape[-1]
    rot = 2 * half
    shape = (1, x.shape[1]) + (1,) * (x.ndim - 3) + (half,)
    c = cos.reshape(shape).astype(x.dtype)
    s = sin.reshape(shape).astype(x.dtype)
    x1 = x[..., :half]
    x2 = x[..., half:rot]
    return jnp.concatenate([x1 * c - x2 * s, x2 * c + x1 * s, x[..., rot:]], axis=-1)


def causal_mask(start, seq):
    qpos = start + jnp.arange(Q_BLOCK)
    kpos = jnp.arange(seq)
    return kpos[None, :] <= qpos[:, None]


def masked_softmax(s, mask):
    return jax.nn.softmax(jnp.where(mask, s, -jnp.inf), axis=-1)


def sweep_query_blocks(fn, q_inputs):
    b, s = q_inputs[0].shape[:2]
    nb = s // Q_BLOCK
    blocks = tuple(jnp.moveaxis(a.reshape((b, nb, Q_BLOCK) + a.shape[2:]), 1, 0) for a in q_inputs)
    starts = jnp.arange(nb, dtype=jnp.int32) * Q_BLOCK
    out = lax.map(lambda args: fn(args[0], *args[1]), (starts, blocks))
    out = jnp.moveaxis(out, 0, 1)
    return out.reshape((b, s) + out.shape[3:])


def even_layer(x, norm_g, w_in, q_norm_g, w_uq, kv_norm_g, w_ukv, diff_lambda, subln_g, w_out, lambda_init):
    B, S, _ = x.shape
    h = rms_norm(x, norm_g)
    c_q, c_kv, k_r, gate_a, dq, dk, dv, gate_b = split_cols(h @ w_in, EVEN_COLS)

    cos_a, sin_a = rope_tables(S, MLA_ROPE)
    q = (rms_norm(c_q, q_norm_g) @ w_uq).reshape(B, S, MLA_HEADS, MLA_NOPE + MLA_ROPE)
    q_nope = q[..., :MLA_NOPE]
    q_rope = apply_rope(q[..., MLA_NOPE:], cos_a, sin_a)
    kv = (rms_norm(c_kv, kv_norm_g) @ w_ukv).reshape(B, S, MLA_HEADS, MLA_NOPE + MLA_V)
    k_nope = kv[..., :MLA_NOPE]
    v_a = kv[..., MLA_NOPE:]
    k_rope = apply_rope(k_r, cos_a, sin_a)
    mla_scale = (MLA_NOPE + MLA_ROPE) ** -0.5

    def mla_block(start, qn, qr):
        s = jnp.einsum('bqhd,bkhd->bhqk', qn, k_nope) + jnp.einsum('bqhr,bkr->bhqk', qr, k_rope)
        p = masked_softmax(s.astype(jnp.float32) * mla_scale, causal_mask(start, S))
        return jnp.einsum('bhqk,bkhd->bqhd', p.astype(v_a.dtype), v_a)

    o_a = sweep_query_blocks(mla_block, (q_nope, q_rope)).reshape(B, S, MLA_WIDTH)

    cos_b, sin_b = rope_tables(S, DIFF_DIM // ROPE_FRACTION)
    dq = apply_rope(dq.reshape(B, S, DIFF_HEADS, 2, DIFF_DIM), cos_b, sin_b)
    dk = apply_rope(dk.reshape(B, S, DIFF_HEADS, 2, DIFF_DIM), cos_b, sin_b)
    dv = dv.reshape(B, S, DIFF_HEADS, 2 * DIFF_DIM)
    lp = diff_lambda.astype(jnp.float32)
    lam = jnp.exp(jnp.sum(lp[0] * lp[1])) - jnp.exp(jnp.sum(lp[2] * lp[3])) + lambda_init
    diff_scale = DIFF_DIM ** -0.5

    def diff_block(start, qb):
        s = jnp.einsum('bqhcd,bkhcd->bchqk', qb, dk).astype(jnp.float32) * diff_scale
        p = masked_softmax(s, causal_mask(start, S))
        a = p[:, 0] - lam * p[:, 1]
        return jnp.einsum('bhqk,bkhe->bqhe', a.astype(dv.dtype), dv)

    o_b = sweep_query_blocks(diff_block, (dq,))
    o_b = (rms_norm(o_b, subln_g) * (1.0 - lambda_init)).reshape(B, S, DIFF_WIDTH)

    mixed = jnp.concatenate([o_a * jax.nn.silu(gate_a), o_b * jax.nn.silu(gate_b)], axis=-1)
    return mixed @ w_out


def odd_layer(x, norm_g, w_in, forget_bias, w_out):
    B, S, _ = x.shape
    h = rms_norm(x, norm_g)
    (dsa_q, dsa_k, dsa_v, idx_q, idx_k, idx_w, gate_c,
     fox_q, fox_k, fox_v, fox_f, gate_d) = split_cols(h @ w_in, ODD_COLS)

    cos_c, sin_c = rope_tables(S, DSA_DIM // ROPE_FRACTION)
    cos_i, sin_i = rope_tables(S, IDX_DIM // ROPE_FRACTION)
    q_c = apply_rope(dsa_q.reshape(B, S, DSA_HEADS, DSA_DIM), cos_c, sin_c)
    k_c = apply_rope(dsa_k, cos_c, sin_c)
    v_c = dsa_v
    q_i = apply_rope(idx_q.reshape(B, S, IDX_HEADS, IDX_DIM), cos_i, sin_i)
    k_i = apply_rope(idx_k, cos_i, sin_i)
    w_i = idx_w * ((IDX_HEADS ** -0.5) * (IDX_DIM ** -0.5))
    top_k = min(TOPK_MAX, S // 4)
    dsa_scale = DSA_DIM ** -0.5
    gather_rows = jax.vmap(lambda src, ids: src[ids])

    def dsa_block(start, qb, qib, wib):
        qpos = start + jnp.arange(Q_BLOCK)
        rel = jax.nn.relu(jnp.einsum('bqhd,bkd->bqhk', qib, k_i))
        idx_score = jnp.einsum('bqhk,bqh->bqk', rel, wib).astype(jnp.float32)
        idx_score = jnp.where(causal_mask(start, S), idx_score, -jnp.inf)
        _, sel = lax.top_k(idx_score, top_k)
        valid = sel <= qpos[None, :, None]
        k_sel = gather_rows(k_c, sel)
        v_sel = gather_rows(v_c, sel)
        s = jnp.einsum('bqhd,bqkd->bhqk', qb, k_sel).astype(jnp.float32) * dsa_scale
        p = masked_softmax(s, valid[:, None])
        return jnp.einsum('bhqk,bqkd->bqhd', p.astype(v_sel.dtype), v_sel)

    o_c = sweep_query_blocks(dsa_block, (q_c, q_i, w_i)).reshape(B, S, DSA_WIDTH)

    fq = fox_q.reshape(B, S, FOX_HEADS, FOX_DIM)
    fk = fox_k.reshape(B, S, FOX_HEADS, FOX_DIM)
    fv = fox_v.reshape(B, S, FOX_HEADS, FOX_DIM)
    log_f = jax.nn.log_sigmoid(fox_f.astype(jnp.float32) + forget_bias.astype(jnp.float32))
    cum = jnp.cumsum(log_f, axis=1)
    cum_k = jnp.moveaxis(cum, 1, 2)[:, :, None, :]
    fox_scale = FOX_DIM ** -0.5

    def fox_block(start, qb, cqb):
        s = jnp.einsum('bqhd,bkhd->bhqk', qb, fk).astype(jnp.float32) * fox_scale
        s = s + (jnp.moveaxis(cqb, 1, 2)[..., None] - cum_k)
        p = masked_softmax(s, causal_mask(start, S))
        return jnp.einsum('bhqk,bkhd->bqhd', p.astype(fv.dtype), fv)

    o_d = sweep_query_blocks(fox_block, (fq, cum)).reshape(B, S, FOX_WIDTH)

    mixed = jnp.concatenate([o_c * jax.nn.silu(gate_c), o_d * jax.nn.silu(gate_d)], axis=-1)
    return mixed @ w_out


def setup_inputs(seed: int = 0) -> dict:
    key = jax.random.key(seed)
    ks = jax.random.split(key, 16)
    f32 = jnp.float32

    def normal(k, shape, scale):
        return jax.random.normal(k, shape, f32) * scale

    def gain(k, shape):
        return 1.0 + 0.01 * jax.random.normal(k, shape, f32)

    even_in = sum(EVEN_COLS)
    odd_in = sum(ODD_COLS)
    even_mix = MLA_WIDTH + DIFF_WIDTH
    odd_mix = DSA_WIDTH + FOX_WIDTH
    return {
        'x': jax.random.normal(ks[0], (BATCH, SEQ, D_MODEL), f32),
        'even_norm': gain(ks[1], (N_EVEN, D_MODEL)),
        'even_w_in': normal(ks[2], (N_EVEN, D_MODEL, even_in), D_MODEL ** -0.5),
        'mla_q_norm': gain(ks[3], (N_EVEN, MLA_Q_RANK)),
        'mla_w_uq': normal(ks[4], (N_EVEN, MLA_Q_RANK, MLA_HEADS * (MLA_NOPE + MLA_ROPE)), MLA_Q_RANK ** -0.5),
        'mla_kv_norm': gain(ks[5], (N_EVEN, MLA_KV_RANK)),
        'mla_w_ukv': normal(ks[6], (N_EVEN, MLA_KV_RANK, MLA_HEADS * (MLA_NOPE + MLA_V)), MLA_KV_RANK ** -0.5),
        'diff_lambda': normal(ks[7], (N_EVEN, 4, DIFF_DIM), 0.1),
        'diff_subln': gain(ks[8], (N_EVEN, 2 * DIFF_DIM)),
        'even_w_out': normal(ks[9], (N_EVEN, even_mix, D_MODEL), even_mix ** -0.5),
        'odd_norm': gain(ks[10], (N_ODD, D_MODEL)),
        'odd_w_in': normal(ks[11], (N_ODD, D_MODEL, odd_in), D_MODEL ** -0.5),
        'fox_forget_bias': jax.random.uniform(ks[12], (N_ODD, FOX_HEADS), f32, 1.0, 4.0),
        'odd_w_out': normal(ks[13], (N_ODD, odd_mix, D_MODEL), odd_mix ** -0.5),
        'final_norm': gain(ks[14], (D_MODEL,)),
    }


def reference(x, even_norm, even_w_in, mla_q_norm, mla_w_uq, mla_kv_norm, mla_w_ukv, diff_lambda, diff_subln, even_w_out, odd_norm, odd_w_in, fox_forget_bias, odd_w_out, final_norm):
    h = x
    for layer in range(DEPTH):
        i = layer // 2
        if layer % 2 == 0:
            lambda_init = 0.8 - 0.6 * math.exp(-0.3 * layer)
            h = h + even_layer(h, even_norm[i], even_w_in[i], mla_q_norm[i], mla_w_uq[i], mla_kv_norm[i], mla_w_ukv[i], diff_lambda[i], diff_subln[i], even_w_out[i], lambda_init)
        else:
            h = h + odd_layer(h, odd_norm[i], odd_w_in[i], fox_forget_bias[i], odd_w_out[i])
    return rms_norm(h, final_norm)
```

```python
import math
from contextlib import ExitStack

import numpy as np
import ml_dtypes
import concourse.bass as bass
import concourse.mybir as mybir
from concourse.bass_utils import run_bass_kernel_spmd

F32 = mybir.dt.float32
BF16 = mybir.dt.bfloat16
AF = mybir.ActivationFunctionType
ALU = mybir.AluOpType
AX = mybir.AxisListType
NPBF = ml_dtypes.bfloat16

D_MODEL = 4096
SEQ = 4096
NT = 1024
ROPE_THETA = 500000.0
EPS = 1e-6
NEG = -1.0e30


class Buf:
    __slots__ = ("name", "w", "r", "g", "dsem", "dcnt")

    def __init__(self, name, r=()):
        self.name = name
        self.w = []
        self.r = list(r)
        self.g = []
        self.dsem = None
        self.dcnt = 0


class KB:
    def __init__(self, nc):
        self.nc = nc
        self.eng = {}
        for nm, h in (("pe", nc.tensor), ("act", nc.scalar), ("dve", nc.vector),
                      ("pool", nc.gpsimd), ("sp", nc.sync)):
            self.eng[nm] = dict(h=h, sem=nc.alloc_semaphore("sem_" + nm), cnt=0, waited={})
        self.nsem = 5
        self.ninst = 0
        self.retired = []

    def buf(self, name):
        return Buf(name, self.retired)

    def bufs(self, name, n):
        return [Buf("%s%d" % (name, i), self.retired) for i in range(n)]

    def retire(self, bufs):
        toks = list(self.retired)
        for b in bufs:
            toks += b.w + b.r
        self.retired = self._compact(toks)

    def _wait(self, en, toks):
        e = self.eng[en]
        need = {}
        for (s, v, src) in toks:
            if en == "pe" and src == "pe":
                continue
            k = id(s)
            if e["waited"].get(k, 0) >= v:
                continue
            if k not in need or need[k][1] < v:
                need[k] = (s, v)
        for k, (s, v) in need.items():
            e["h"].wait_ge(s, v)
            e["waited"][k] = v
            self.ninst += 1

    @staticmethod
    def _compact(toks):
        best = {}
        for (s, v, src) in toks:
            k = id(s)
            if k not in best or best[k][1] < v:
                best[k] = (s, v, src)
        return list(best.values())

    def _deps(self, reads, writes, acc):
        toks = []
        for b in reads:
            toks += b.w
        for b in writes:
            toks += (b.g if acc else b.w)
            toks += b.r
        return toks

    def _commit(self, tok, reads, writes, acc):
        for b in reads:
            b.r.append(tok)
            if len(b.r) > 16:
                b.r = self._compact(b.r)
        for b in writes:
            if acc:
                b.w.append(tok)
                if len(b.w) > 16:
                    b.w = self._compact(b.w)
            else:
                b.g = self._compact(b.w + b.r)
                b.w = [tok]
            b.r = []

    def op(self, en, fn, reads=(), writes=(), acc=False):
        e = self.eng[en]
        self._wait(en, self._deps(reads, writes, acc))
        ins = fn(e["h"])
        e["cnt"] += 1
        ins.then_inc(e["sem"], 1)
        self._commit((e["sem"], e["cnt"], en), reads, writes, acc)
        self.ninst += 1
        return ins

    def dma(self, q, out_ap, in_ap, reads=(), writes=(), sembuf=None, acc=False, store=False, **kw):
        e = self.eng[q]
        sb = sembuf if sembuf is not None else (reads[0] if store else writes[0])
        if sb.dsem is None:
            sb.dsem = self.nc.alloc_semaphore("dsem%d" % self.nsem)
            self.nsem += 1
        self._wait(q, self._deps(reads, writes, acc))
        ins = e["h"].dma_start(out=out_ap, in_=in_ap, **kw)
        sb.dcnt += 16
        ins.then_inc(sb.dsem, 16)
        self._commit((sb.dsem, sb.dcnt, "dma"), reads, writes, acc)
        self.ninst += 1
        return ins

    def finish(self, bufs, en="sp"):
        toks = []
        for b in bufs:
            toks += b.w
        self._wait(en, toks)


def tile_w(W, col_lists):
    Kd = W.shape[0]
    KC = Kd // 128
    out = np.zeros((len(col_lists), 128, KC, 128), np.float32)
    for b, cols in enumerate(col_lists):
        cols = np.asarray(cols)
        valid = cols >= 0
        blk = np.zeros((Kd, 128), np.float32)
        blk[:, valid] = W[:, cols[valid]]
        out[b] = blk.reshape(KC, 128, 128).transpose(1, 0, 2)
    return out


def tile_w_wide(W, ncol):
    Kd, C = W.shape
    KC = Kd // 128
    return np.ascontiguousarray(W.reshape(KC, 128, C // ncol, ncol).transpose(2, 1, 0, 3))


def rng_(a, n):
    return list(range(a, a + n))


def core_positions(j):
    t = np.arange(NT)
    return ((4 * (t // 128) + j) * 128 + (t % 128)).astype(np.int64)


def rope_table(pos, group, rot):
    half = rot // 2
    inv = (np.float32(ROPE_THETA) ** (-(np.arange(0, rot, 2, dtype=np.float32)) / np.float32(rot))).astype(np.float32)
    ang = (pos.astype(np.float32)[None, :] * inv[:, None]).astype(np.float32)
    cos = np.cos(ang.astype(np.float64)).astype(np.float32)
    sin = np.sin(ang.astype(np.float64)).astype(np.float32)
    C = np.ones((128, NT), np.float32)
    S = np.zeros((128, NT), np.float32)
    R = np.zeros((128, 128), np.float32)
    for g0 in range(0, 128, group):
        for i in range(half):
            C[g0 + i] = cos[i]
            C[g0 + half + i] = cos[i]
            S[g0 + i] = -sin[i]
            S[g0 + half + i] = sin[i]
            R[g0 + half + i, g0 + i] = 1.0
            R[g0 + i, g0 + half + i] = 1.0
    return C, S, R


def diag_masks(j):
    m = np.zeros((128, 4, 128), np.float32)
    k = np.arange(128)[:, None]
    q = np.arange(128)[None, :]
    for jp in range(4):
        if jp < j:
            m[:, jp, :] = 1.0
        elif jp == j:
            m[:, jp, :] = (k <= q).astype(np.float32)
    return m.astype(NPBF)


class Ctx:
    def __init__(self):
        self.nc = bass.Bass("TRN2", target_bir_lowering=False)
        self.K = KB(self.nc)
        self.es = ExitStack()
        self.outs = []

    def sb(self, name, shape, dt, es=None):
        return (es or self.es).enter_context(self.nc.sbuf_tensor(name, shape, dt))

    def ps(self, name, shape, dt=F32, es=None):
        return (es or self.es).enter_context(self.nc.psum_tensor(name, shape, dt))

    def din(self, name, shape, dt):
        return self.nc.dram_tensor(name, list(shape), dt, kind="ExternalInput").ap()

    def dout(self, name, shape, dt):
        ap = self.nc.dram_tensor(name, list(shape), dt, kind="ExternalOutput").ap()
        b = self.K.buf(name)
        self.outs.append(b)
        return ap, b


def make_consts(cx):
    nc, K = cx.nc, cx.K
    c = {}
    c["idf"] = cx.sb("c_idf", [128, 128], F32)
    c["idb"] = cx.sb("c_idb", [128, 128], BF16)
    c["onef"] = cx.sb("c_onef", [128, 128], F32)
    c["oneb"] = cx.sb("c_oneb", [128, 128], BF16)
    c["eps"] = cx.sb("c_eps", [128, 1], F32)
    c["one1"] = cx.sb("c_one1", [128, 1], F32)
    c["B"] = K.buf("consts")
    B = c["B"]
    K.op("pool", lambda e: e.memset(c["idf"][:], 0.0), writes=[B])
    K.op("pool", lambda e: e.affine_select(out=c["idf"][:], in_=c["idf"][:], pattern=[[-1, 128]],
                                           compare_op=ALU.not_equal, fill=1.0, base=0,
                                           channel_multiplier=1), reads=[B], writes=[B])
    K.op("pool", lambda e: e.memset(c["onef"][:], 1.0), reads=[B], writes=[B])
    K.op("pool", lambda e: e.memset(c["oneb"][:], 1.0), reads=[B], writes=[B])
    K.op("pool", lambda e: e.memset(c["eps"][:], EPS), reads=[B], writes=[B])
    K.op("pool", lambda e: e.memset(c["one1"][:], 1.0), reads=[B], writes=[B])
    K.op("dve", lambda e: e.tensor_copy(out=c["idb"][:], in_=c["idf"][:]), reads=[B], writes=[B])
    return c


def norm_transpose(cx, c, x_ap, g_sb, Bg, hT, BhT):
    nc, K = cx.nc, cx.K
    with ExitStack() as es:
        xt = [cx.sb("n_xt%d" % i, [128, D_MODEL], F32, es) for i in range(2)]
        xn = [cx.sb("n_xn%d" % i, [128, D_MODEL], BF16, es) for i in range(2)]
        junk = cx.sb("n_junk", [128, D_MODEL], BF16, es)
        ss = [cx.sb("n_ss%d" % i, [128, 1], F32, es) for i in range(2)]
        ptr = [cx.ps("n_ptr%d" % i, [128, 1024], BF16, es) for i in range(2)]
        Bxt, Bxn, Bss, Bptr = K.bufs("xt", 2), K.bufs("xn", 2), K.bufs("ss", 2), K.bufs("ptr", 2)
        Bjunk = K.buf("junk")
        for s in range(NT // 128):
            p = s % 2
            K.dma("sp", xt[p][:], x_ap[s * 128:(s + 1) * 128, :], writes=[Bxt[p]])
            K.op("act", lambda e: e.activation(out=junk[:], in_=xt[p][:], func=AF.Square,
                                               accum_out=ss[p][:, 0:1]),
                 reads=[Bxt[p]], writes=[Bjunk, Bss[p]])
            K.op("act", lambda e: e.activation(out=ss[p][:], in_=ss[p][:], func=AF.Sqrt,
                                               scale=1.0 / D_MODEL, bias=c["eps"][:, 0:1]),
                 reads=[Bss[p], c["B"]], writes=[Bss[p]])
            K.op("dve", lambda e: e.reciprocal(out=ss[p][:], in_=ss[p][:]),
                 reads=[Bss[p]], writes=[Bss[p]])
            K.op("act", lambda e: e.activation(out=xn[p][:], in_=xt[p][:], func=AF.Copy,
                                               scale=ss[p][:, 0:1]),
                 reads=[Bxt[p], Bss[p]], writes=[Bxn[p]])
            for q in range(4):
                pp = (s * 4 + q) % 2
                for u in range(8):
                    kc = q * 8 + u
                    K.op("pe", lambda e: e.transpose(out=ptr[pp][:, u * 128:(u + 1) * 128],
                                                     in_=xn[p][:, kc * 128:(kc + 1) * 128],
                                                     identity=c["idb"][:]),
                         reads=[Bxn[p], c["B"]], writes=[Bptr[pp]], acc=(u > 0))
                gin = g_sb[:, q * 8:(q + 1) * 8]
                K.op("dve", lambda e: e.tensor_tensor(
                    out=hT[:, q * 8:(q + 1) * 8, s * 128:(s + 1) * 128],
                    in0=ptr[pp][:].rearrange("p (u t) -> p u t", u=8),
                    in1=gin.unsqueeze(2).to_broadcast([128, 8, 128]),
                    op=ALU.mult),
                    reads=[Bptr[pp], Bg], writes=[BhT], acc=True)
        K.retire(Bxt + Bxn + Bss + Bptr + [Bjunk])


class ProjState:
    pass


def proj_setup(cx, c, n_rope_types):
    nc, K = cx.nc, cx.K
    P = ProjState()
    P.wsl = [cx.sb("wsl%d" % i, [128, 32, 128], BF16) for i in range(3)]
    P.Bw = K.bufs("wsl", 3)
    P.pacc = [cx.ps("pacc%d" % i, [128, 512], F32) for i in range(4)]
    P.Bpacc = K.bufs("pacc", 4)
    P.paux = [cx.ps("paux%d" % i, [128, 512], F32) for i in range(2)]
    P.Bpaux = K.bufs("paux", 2)
    P.ptr = cx.ps("ptrv", [128, 1024], BF16)
    P.Bptr = K.buf("ptrv")
    P.st_bf = [cx.sb("st_bf%d" % i, [128, NT], BF16) for i in range(2)]
    P.Bst_bf = K.bufs("st_bf", 2)
    P.st_f = [cx.sb("st_f%d" % i, [128, NT], F32) for i in range(2)]
    P.Bst_f = K.bufs("st_f", 2)
    P.xf = [cx.sb("xf%d" % i, [128, 512], F32) for i in range(2)]
    P.Bxf = K.bufs("xf", 2)
    P.tmp = [cx.sb("tmp%d" % i, [128, 512], F32) for i in range(2)]
    P.Btmp = K.bufs("tmp", 2)
    P.t2 = [cx.sb("t2%d" % i, [128, 512], F32) for i in range(2)]
    P.Bt2 = K.bufs("t2", 2)
    P.vt = [cx.sb("vt%d" % i, [128, 8, 128], BF16) for i in range(2)]
    P.Bvt = K.bufs("vt", 2)
    P.ropeC = [cx.sb("ropeC%d" % i, [128, NT], F32) for i in range(n_rope_types)]
    P.ropeS = [cx.sb("ropeS%d" % i, [128, NT], F32) for i in range(n_rope_types)]
    P.ropeR = [cx.sb("ropeR%d" % i, [128, 128], BF16) for i in range(n_rope_types)]
    P.hi = [cx.sb("hi%d" % i, [128, 512], BF16) for i in range(2)]
    P.Bhi = K.bufs("hi", 2)
    P.lo = [cx.sb("lo%d" % i, [128, 512], BF16) for i in range(2)]
    P.Blo = K.bufs("lo", 2)
    P.Brope = K.buf("rope")
    P.BropeR = K.buf("ropeR")
    P.nblk = 0
    P.nrope = 0
    return P


def load_rope(cx, P, i, C_ap, S_ap, R_ap):
    K = cx.K
    K.dma("sp", P.ropeC[i][:], C_ap, writes=[P.Brope], acc=True)
    K.dma("sp", P.ropeS[i][:], S_ap, writes=[P.Brope], acc=True)
    K.dma("pool", P.ropeR[i][:], R_ap, writes=[P.BropeR], acc=True)


def proj_block(cx, c, P, w_ap, KC, rhs, Brhs, post):
    nc, K = cx.nc, cx.K
    bi = P.nblk
    if bi >= getattr(P, "limit", 10**9):
        return
    P.nblk += 1
    sl = bi % 3
    K.dma("pool", P.wsl[sl][:, 0:KC, :], w_ap, writes=[P.Bw[sl]])
    for t in range(2):
        bk = (bi % 2) * 2 + t
        for kc in range(KC):
            K.op("pe", lambda e: e.matmul(P.pacc[bk][:], lhsT=P.wsl[sl][:, kc, :],
                                          rhs=rhs[:, kc, t * 512:(t + 1) * 512],
                                          start=(kc == 0), stop=(kc == KC - 1)),
                 reads=[P.Bw[sl], Brhs], writes=[P.Bpacc[bk]])
        post(bi, t, P.pacc[bk], P.Bpacc[bk])
    post(bi, None, None, None)


def post_store_bf16(cx, P, dst_ap, Bdst, scale=None, Bscale=None, act_func=None):
    K = cx.K

    def post(bi, t, ps, Bps):
        p = bi % 2
        if t is None:
            K.dma("sp", dst_ap, P.st_bf[p][:], reads=[P.Bst_bf[p]], writes=[Bdst], acc=True, store=True)
            return
        o = P.st_bf[p][:, t * 512:(t + 1) * 512]
        if scale is None:
            K.op("act", lambda e: e.activation(out=o, in_=ps[:], func=AF.Copy),
                 reads=[Bps], writes=[P.Bst_bf[p]], acc=(t > 0))
        else:
            K.op("dve", lambda e: e.tensor_tensor(out=o, in0=ps[:], in1=scale[:, t * 512:(t + 1) * 512],
                                                  op=ALU.mult),
                 reads=[Bps, Bscale], writes=[P.Bst_bf[p]], acc=(t > 0))
    return post


def post_store_f32(cx, P, dst_ap, Bdst, func):
    K = cx.K

    def post(bi, t, ps, Bps):
        p = bi % 2
        if t is None:
            K.dma("sp", dst_ap, P.st_f[p][:], reads=[P.Bst_f[p]], writes=[Bdst], acc=True, store=True)
            return
        o = P.st_f[p][:, t * 512:(t + 1) * 512]
        K.op("act", lambda e: e.activation(out=o, in_=ps[:], func=func),
             reads=[Bps], writes=[P.Bst_f[p]], acc=(t > 0))
    return post


def rope_tile(cx, c, P, ri, t, ps, Bps, out_ap, Bout, acc, scale=None, Bscale=None, rows=128):
    K = cx.K
    u = P.nrope % 2
    P.nrope += 1
    cs = slice(t * 512, (t + 1) * 512)
    if scale is None:
        xs, Bxs = ps, Bps
    else:
        K.op("dve", lambda e: e.tensor_tensor(out=P.xf[u][:], in0=ps[:], in1=scale[:, cs], op=ALU.mult),
             reads=[Bps, Bscale], writes=[P.Bxf[u]])
        xs, Bxs = P.xf[u], P.Bxf[u]
    K.op("dve", lambda e: e.tensor_copy(out=P.hi[u][:], in_=xs[:]), reads=[Bxs], writes=[P.Bhi[u]])
    K.op("dve", lambda e: e.tensor_tensor(out=P.lo[u][:], in0=xs[:], in1=P.hi[u][:], op=ALU.subtract),
         reads=[Bxs, P.Bhi[u]], writes=[P.Blo[u]])
    K.op("pe", lambda e: e.matmul(P.paux[u][:], lhsT=P.ropeR[ri][:], rhs=P.hi[u][:], start=True, stop=False),
         reads=[P.Bhi[u], P.BropeR], writes=[P.Bpaux[u]])
    K.op("pe", lambda e: e.matmul(P.paux[u][:], lhsT=P.ropeR[ri][:], rhs=P.lo[u][:], start=False, stop=True),
         reads=[P.Blo[u], P.BropeR], writes=[P.Bpaux[u]])
    K.op("dve", lambda e: e.tensor_tensor(out=P.tmp[u][:], in0=xs[:], in1=P.ropeC[ri][:, cs], op=ALU.mult),
         reads=[Bxs, P.Brope], writes=[P.Btmp[u]])
    K.op("dve", lambda e: e.tensor_tensor(out=P.t2[u][:], in0=P.paux[u][:], in1=P.ropeS[ri][:, cs], op=ALU.mult),
         reads=[P.Bpaux[u], P.Brope], writes=[P.Bt2[u]])
    K.op("dve", lambda e: e.tensor_tensor(out=out_ap, in0=P.t2[u][0:rows, :], in1=P.tmp[u][0:rows, :], op=ALU.add),
         reads=[P.Bt2[u], P.Btmp[u]], writes=[Bout], acc=acc)


def post_rope(cx, c, P, ri, dst_ap, Bdst, scale=None, Bscale=None):
    K = cx.K

    def post(bi, t, ps, Bps):
        p = bi % 2
        if t is None:
            K.dma("sp", dst_ap, P.st_bf[p][:], reads=[P.Bst_bf[p]], writes=[Bdst], acc=True, store=True)
            return
        rope_tile(cx, c, P, ri, t, ps, Bps, P.st_bf[p][:, t * 512:(t + 1) * 512], P.Bst_bf[p], t > 0,
                  scale, Bscale)
    return post


def post_transpose(cx, c, P, dst_ap, Bdst, scale=None, Bscale=None):
    K = cx.K

    def post(bi, t, ps, Bps):
        p = bi % 2
        if t is not None:
            o = P.st_bf[p][:, t * 512:(t + 1) * 512]
            if scale is None:
                K.op("act", lambda e: e.activation(out=o, in_=ps[:], func=AF.Copy),
                     reads=[Bps], writes=[P.Bst_bf[p]], acc=(t > 0))
            else:
                K.op("dve", lambda e: e.tensor_tensor(out=o, in0=ps[:], in1=scale[:, t * 512:(t + 1) * 512],
                                                      op=ALU.mult),
                     reads=[Bps, Bscale], writes=[P.Bst_bf[p]], acc=(t > 0))
            return
        for s in range(8):
            K.op("pe", lambda e: e.transpose(out=P.ptr[:, s * 128:(s + 1) * 128],
                                             in_=P.st_bf[p][:, s * 128:(s + 1) * 128], identity=c["idb"][:]),
                 reads=[P.Bst_bf[p], c["B"]], writes=[P.Bptr], acc=(s > 0))
        K.op("act", lambda e: e.activation(out=P.vt[p][:], in_=P.ptr[:].rearrange("p (s d) -> p s d", s=8),
                                           func=AF.Copy),
             reads=[P.Bptr], writes=[P.Bvt[p]])
        K.dma("sp", dst_ap, P.vt[p][:], reads=[P.Bvt[p]], writes=[Bdst], acc=True, store=True)
    return post


def post_stat(cx, c, P, gsb, Bg, blk, nblk, dstg, Bdstg, n_feat, rstd, Brstd):
    K = cx.K

    def post(bi, t, ps, Bps):
        if t is None:
            return
        cs = slice(t * 512, (t + 1) * 512)
        K.op("dve", lambda e: e.tensor_scalar(out=dstg[:, blk, cs], in0=ps[:], scalar1=gsb[:, blk:blk + 1],
                                              scalar2=None, op0=ALU.mult),
             reads=[Bps, Bg], writes=[Bdstg], acc=True)
        u = P.nrope % 2
        P.nrope += 1
        K.op("act", lambda e: e.activation(out=P.hi[u][:], in_=ps[:], func=AF.Square),
             reads=[Bps, Bdstg], writes=[P.Bhi[u]])
        K.op("pe", lambda e: e.matmul(P.paux[t][:], lhsT=c["oneb"][:], rhs=P.hi[u][:],
                                      start=(blk == 0), stop=(blk == nblk - 1)),
             reads=[P.Bhi[u], c["B"]], writes=[P.Bpaux[t]])
        if blk == nblk - 1:
            K.op("act", lambda e: e.activation(out=rstd[:, cs], in_=P.paux[t][:], func=AF.Sqrt,
                                               scale=1.0 / n_feat, bias=c["eps"][:, 0:1]),
                 reads=[P.Bpaux[t], c["B"]], writes=[Brstd], acc=(t > 0))
            K.op("dve", lambda e: e.reciprocal(out=rstd[:, cs], in_=rstd[:, cs]),
                 reads=[Brstd], writes=[Brstd])
    return post


EVEN_BLOCKS = 8 + 4 + 1 + 16 * 5


def even_col_lists():
    L = []
    for b in range(8):
        L.append(rng_(b * 128, 128))
    for b in range(4):
        L.append(rng_(1024 + b * 128, 128))
    L.append(rng_(1536, 64) + rng_(1536, 64))
    for name_off in (1600, 3648, 5696, 7744, 9792):
        for b in range(16):
            L.append(rng_(name_off + b * 128, 128))
    return L


def uq_col_lists():
    L = []
    for h in range(16):
        L.append(rng_(h * 192, 128))
    for i in range(8):
        L.append(rng_((2 * i) * 192 + 128, 64) + rng_((2 * i + 1) * 192 + 128, 64))
    return L


def ukv_col_lists():
    L = []
    for h in range(16):
        L.append(rng_(h * 256, 128))
    for h in range(16):
        L.append(rng_(h * 256 + 128, 128))
    return L


def build_A_even(limit=10**9):
    cx = Ctx()
    nc, K = cx.nc, cx.K
    x = cx.din("x", [NT, D_MODEL], F32)
    g0 = cx.din("g0", [128, 32], F32)
    win = cx.din("win", [min(EVEN_BLOCKS, max(limit, 1)), 128, 32, 128], F32)
    NW = min(EVEN_BLOCKS, max(limit, 1))
    gq = cx.din("gq", [128, 8], F32)
    wuq = cx.din("wuq", [24, 128, 8, 128], F32)
    gkv = cx.din("gkv", [128, 4], F32)
    wukv = cx.din("wukv", [32, 128, 4, 128], F32)
    rA = [cx.din("ropeA_" + n, sh, F32) for n, sh in (("C", [128, NT]), ("S", [128, NT]), ("R", [128, 128]))]
    rB = [cx.din("ropeB_" + n, sh, F32) for n, sh in (("C", [128, NT]), ("S", [128, NT]), ("R", [128, 128]))]
    qa_nope, Bqan = cx.dout("qa_nope", [16, 128, NT], BF16)
    qa_rope, Bqar = cx.dout("qa_rope", [8, 128, NT], BF16)
    ka_nope, Bkan = cx.dout("ka_nope", [16, 128, NT], BF16)
    ka_rope, Bkar = cx.dout("ka_rope", [128, NT], BF16)
    va, Bva = cx.dout("va", [16, 128, 8, 128], BF16)
    ga, Bga = cx.dout("ga", [16, 128, NT], F32)
    dq, Bdq = cx.dout("dq", [16, 128, NT], BF16)
    dk, Bdk = cx.dout("dk", [16, 128, NT], BF16)
    dv, Bdv = cx.dout("dv", [16, 128, 8, 128], BF16)
    gb, Bgb = cx.dout("gb", [16, 128, NT], F32)

    with cx.es:
        c = make_consts(cx)
        gs = cx.sb("gs", [128, 32], F32)
        gqs = cx.sb("gqs", [128, 8], F32)
        gkvs = cx.sb("gkvs", [128, 4], F32)
        Bgs = K.buf("gs")
        K.dma("sp", gs[:], g0, writes=[Bgs], acc=True)
        K.dma("sp", gqs[:], gq, writes=[Bgs], acc=True)
        K.dma("sp", gkvs[:], gkv, writes=[Bgs], acc=True)
        hT = cx.sb("hT", [128, 32, NT], BF16)
        BhT = K.buf("hT")
        norm_transpose(cx, c, x, gs, Bgs, hT, BhT)
        P = proj_setup(cx, c, 2)
        P.hT, P.BhT = hT, BhT
        P.limit = limit
        load_rope(cx, P, 0, *rA)
        load_rope(cx, P, 1, *rB)
        cqg = cx.sb("cqg", [128, 8, NT], BF16)
        ckvg = cx.sb("ckvg", [128, 4, NT], BF16)
        rstd_q = cx.sb("rstd_q", [128, NT], F32)
        rstd_kv = cx.sb("rstd_kv", [128, NT], F32)
        Bcqg, Bckvg, Brq, Brkv = K.buf("cqg"), K.buf("ckvg"), K.buf("rq"), K.buf("rkv")

        bi = 0
        for b in range(8):
            proj_block(cx, c, P, win[min(bi, NW - 1)], 32, P.hT, P.BhT,
                       post_stat(cx, c, P, gqs, Bgs, b, 8, cqg, Bcqg, 1024, rstd_q, Brq))
            bi += 1
        for b in range(4):
            proj_block(cx, c, P, win[min(bi, NW - 1)], 32, P.hT, P.BhT,
                       post_stat(cx, c, P, gkvs, Bgs, b, 4, ckvg, Bckvg, 512, rstd_kv, Brkv))
            bi += 1
        for h in range(16):
            proj_block(cx, c, P, wuq[h], 8, cqg, Bcqg,
                       post_store_bf16(cx, P, qa_nope[h], Bqan, rstd_q, Brq))
        for i in range(8):
            proj_block(cx, c, P, wuq[16 + i], 8, cqg, Bcqg,
                       post_rope(cx, c, P, 0, qa_rope[i], Bqar, rstd_q, Brq))
        for h in range(16):
            proj_block(cx, c, P, wukv[h], 4, ckvg, Bckvg,
                       post_store_bf16(cx, P, ka_nope[h], Bkan, rstd_kv, Brkv))
        for h in range(16):
            proj_block(cx, c, P, wukv[16 + h], 4, ckvg, Bckvg,
                       post_transpose(cx, c, P, va[h], Bva, rstd_kv, Brkv))
        proj_block(cx, c, P, win[min(bi, NW - 1)], 32, P.hT, P.BhT, post_rope(cx, c, P, 0, ka_rope, Bkar))
        bi += 1
        for b in range(16):
            proj_block(cx, c, P, win[min(bi, NW - 1)], 32, P.hT, P.BhT, post_store_f32(cx, P, ga[b], Bga, AF.Silu))
            bi += 1
        for b in range(16):
            proj_block(cx, c, P, win[min(bi, NW - 1)], 32, P.hT, P.BhT, post_rope(cx, c, P, 1, dq[b], Bdq))
            bi += 1
        for b in range(16):
            proj_block(cx, c, P, win[min(bi, NW - 1)], 32, P.hT, P.BhT, post_rope(cx, c, P, 1, dk[b], Bdk))
            bi += 1
        for b in range(16):
            proj_block(cx, c, P, win[min(bi, NW - 1)], 32, P.hT, P.BhT, post_transpose(cx, c, P, dv[b], Bdv))
            bi += 1
        for b in range(16):
            proj_block(cx, c, P, win[min(bi, NW - 1)], 32, P.hT, P.BhT, post_store_f32(cx, P, gb[b], Bgb, AF.Silu))
            bi += 1
        K.finish(cx.outs)
    return cx


def host_A_even(inp, cores):
    win = tile_w(inp["even_w_in"][0], even_col_lists())
    wuq = tile_w(inp["mla_w_uq"][0], uq_col_lists())
    wukv = tile_w(inp["mla_w_ukv"][0], ukv_col_lists())
    g0 = np.ascontiguousarray(inp["even_norm"][0].reshape(32, 128).T)
    gq = np.ascontiguousarray(inp["mla_q_norm"][0].reshape(8, 128).T)
    gkv = np.ascontiguousarray(inp["mla_kv_norm"][0].reshape(4, 128).T)
    maps = []
    for r in cores:
        b, j = divmod(r, 4)
        pos = core_positions(j)
        CA, SA, RA = rope_table(pos, 64, 64)
        CB, SB, RB = rope_table(pos, 128, 32)
        maps.append({"x": np.ascontiguousarray(inp["x"][b][pos]), "g0": g0, "win": win, "gq": gq, "wuq": wuq,
                     "gkv": gkv, "wukv": wukv, "ropeA_C": CA, "ropeA_S": SA, "ropeA_R": RA,
                     "ropeB_C": CB, "ropeB_S": SB, "ropeB_R": RB})
    return maps


class AttnState:
    pass


def attn_setup(cx, c, n_ech, mask_ap):
    nc, K = cx.nc, cx.K
    A = AttnState()
    A.n_ech = n_ech
    A.kT = [cx.sb("kT%d" % i, [128, 4, NT], BF16) for i in range(2)]
    A.BkT = K.bufs("kT", 2)
    A.vS = [cx.sb("vS%d" % i, [128, n_ech, 32, 128], BF16) for i in range(2)]
    A.BvS = K.bufs("vS", 2)
    A.qT = [cx.sb("qT%d" % i, [128, NT], BF16) for i in range(2)]
    A.BqT = K.bufs("qT", 2)
    A.gt = [cx.sb("gt%d" % i, [128, NT], F32) for i in range(2)]
    A.Bgt = K.bufs("gt", 2)
    A.pt = [cx.sb("pt%d" % i, [128, 512], BF16) for i in range(3)]
    A.Bpt = K.bufs("pt", 3)
    A.rden = cx.sb("rden", [128, 512], F32)
    A.Brden = K.buf("rden")
    A.osb = [cx.sb("osb%d" % i, [128, 512], F32) for i in range(2)]
    A.Bosb = K.bufs("osb", 2)
    A.mask = cx.sb("mask_sb", [128, 4, 128], BF16)
    A.Bmask = K.buf("mask")
    K.dma("sp", A.mask[:], mask_ap, writes=[A.Bmask])
    A.ps = [cx.ps("ps%d" % i, [128, 512], F32) for i in range(2)]
    A.Bps = K.bufs("ps", 2)
    A.po = [[cx.ps("po%d_%d" % (i, e), [128, 512], F32) for e in range(n_ech)] for i in range(2)]
    A.Bpo = [K.bufs("po%d_" % i, n_ech) for i in range(2)]
    A.pd = [cx.ps("pd%d" % i, [128, 512], F32) for i in range(2)]
    A.Bpd = K.bufs("pd", 2)
    A.nitem = 0
    A.ngrp = 0
    return A


def attn_core(cx, c, A, parts, Bparts, v_sb, Bv, scale, finalize, exp_fn=None, mask_fn=None, n_ech=1):
    K = cx.K
    for g in range(2):
        pb = A.ngrp % 2
        A.ngrp += 1
        nkc = 16 * g + 16
        for kc in range(nkc):
            jp, sp = kc % 4, kc // 4
            s_lo = max(4 * g, sp)
            c0 = (s_lo - 4 * g) * 128
            it = A.nitem
            A.nitem += 1
            psb, Bpsb = A.ps[it % 2], A.Bps[it % 2]
            pt, Bpt = A.pt[it % 3], A.Bpt[it % 3]
            for pi, (kfn, qfn) in enumerate(parts):
                K.op("pe", lambda e: e.matmul(psb[:, c0:512], lhsT=kfn(jp, sp),
                                              rhs=qfn(4 * g * 128 + c0, (4 * g + 4) * 128),
                                              start=(pi == 0), stop=(pi == len(parts) - 1)),
                     reads=Bparts, writes=[Bpsb])
            if exp_fn is None:
                K.op("act", lambda e: e.activation(out=pt[:, c0:512], in_=psb[:, c0:512], func=AF.Exp,
                                                   scale=scale),
                     reads=[Bpsb], writes=[Bpt])
            else:
                exp_fn(g, kc, c0, psb, Bpsb, pt, Bpt)
            if mask_fn is not None:
                mask_fn(g, kc, c0, pt, Bpt)
            elif sp >= 4 * g:
                K.op("dve", lambda e: e.tensor_tensor(out=pt[:, c0:c0 + 128], in0=pt[:, c0:c0 + 128],
                                                      in1=A.mask[:, jp, :], op=ALU.mult),
                     reads=[Bpt, A.Bmask], writes=[Bpt])
            for ec in range(n_ech):
                K.op("pe", lambda e: e.matmul(A.po[pb][ec][:, c0:512],
                                              lhsT=v_sb[:, ec, jp * 8 + sp, :],
                                              rhs=pt[:, c0:512], start=(kc == 0), stop=(kc == nkc - 1)),
                     reads=[Bv, Bpt], writes=[A.Bpo[pb][ec]])
            K.op("pe", lambda e: e.matmul(A.pd[pb][:, c0:512], lhsT=c["oneb"][:], rhs=pt[:, c0:512],
                                          start=(kc == 0), stop=(kc == nkc - 1)),
                 reads=[Bpt, c["B"]], writes=[A.Bpd[pb]])
        finalize(g, A.po[pb], A.Bpo[pb], A.pd[pb], A.Bpd[pb])


def load_head(cx, A, sl, k_src, v_src, q_src, g_src):
    K = cx.K
    if k_src is not None:
        K.dma("sp", A.kT[sl][:], k_src.rearrange("j p t -> p j t"), writes=[A.BkT[sl]])
    for ec, vs in enumerate(v_src or []):
        K.dma("sp", A.vS[sl][:, ec].rearrange("p (j s) d -> p j (s d)", j=4),
              vs.rearrange("j p s d -> p j (s d)"), writes=[A.BvS[sl]], acc=(ec > 0))
    if q_src is not None:
        K.dma("sp", A.qT[sl][:], q_src, writes=[A.BqT[sl]])
    if g_src is not None:
        K.dma("sp", A.gt[sl][:], g_src, writes=[A.Bgt[sl]])


def out_proj(cx, c, mixT, BmixT, wout, x_ap, o_ap, Bo, final_g=None):
    nc, K = cx.nc, cx.K
    with ExitStack() as es:
        wsl = [cx.sb("wo%d" % i, [128, 32, 256], BF16, es) for i in range(2)]
        Bw = K.bufs("wo", 2)
        xr = [cx.sb("xr%d" % i, [128, 256], F32, es) for i in range(3)]
        Bxr = K.bufs("xr", 3)
        sqj = cx.sb("sqj", [128, 256], BF16, es)
        Bsqj = K.buf("sqj")
        ssq = cx.sb("ssq", [128, 8, 16], F32, es)
        Bssq = K.buf("ssq")
        pacc = [cx.ps("po_acc%d" % i, [128, 256], F32, es) for i in range(4)]
        Bpacc = K.bufs("po_acc", 4)
        n = 0
        for cb in range(16):
            sl = cb % 2
            K.dma("pool", wsl[sl][:], wout[cb], writes=[Bw[sl]])
            for tt in range(8):
                bk = n % 4
                xs = n % 3
                n += 1
                K.dma("sp", xr[xs][:], x_ap[tt * 128:(tt + 1) * 128, cb * 256:(cb + 1) * 256], writes=[Bxr[xs]])
                for kc in range(32):
                    K.op("pe", lambda e: e.matmul(pacc[bk][:], lhsT=mixT[:, kc, tt * 128:(tt + 1) * 128],
                                                  rhs=wsl[sl][:, kc, :], start=(kc == 0), stop=(kc == 31)),
                         reads=[BmixT, Bw[sl]], writes=[Bpacc[bk]])
                K.op("dve", lambda e: e.tensor_tensor(out=xr[xs][:], in0=pacc[bk][:], in1=xr[xs][:], op=ALU.add),
                     reads=[Bpacc[bk], Bxr[xs]], writes=[Bxr[xs]])
                if final_g is not None:
                    K.op("act", lambda e: e.activation(out=sqj[:], in_=xr[xs][:], func=AF.Square,
                                                       accum_out=ssq[:, tt, cb:cb + 1]),
                         reads=[Bxr[xs]], writes=[Bsqj, Bssq])
                K.dma("sp", o_ap[tt * 128:(tt + 1) * 128, cb * 256:(cb + 1) * 256], xr[xs][:],
                      reads=[Bxr[xs]], writes=[Bo], acc=True, store=True)
        if final_g is None:
            K.retire(Bw + Bxr + Bpacc + [Bsqj, Bssq])
            return
        gb = cx.sb("fin_g", [128, D_MODEL], F32, es)
        Bgb = K.buf("fin_g")
        K.dma("sp", gb[:], final_g.to_broadcast([128, D_MODEL]), writes=[Bgb])
        rs = cx.sb("fin_rs", [128, 8], F32, es)
        Brs = K.buf("fin_rs")
        K.op("dve", lambda e: e.tensor_reduce(out=rs[:], in_=ssq[:], axis=AX.X, op=ALU.add),
             reads=[Bssq], writes=[Brs])
        K.op("act", lambda e: e.activation(out=rs[:], in_=rs[:], func=AF.Sqrt, scale=1.0 / D_MODEL,
                                           bias=c["eps"][:, 0:1]), reads=[Brs, c["B"]], writes=[Brs])
        K.op("dve", lambda e: e.reciprocal(out=rs[:], in_=rs[:]), reads=[Brs], writes=[Brs])
        Bo2 = K.buf("o_final")
        Bfin = K.bufs("fin_sem", 2)
        cx.outs.append(Bo2)
        for tt in range(8):
            for hh in range(2):
                bsl = (tt * 2 + hh) % 2
                cs = slice(hh * 2048, (hh + 1) * 2048)
                fb = wsl[bsl][:].rearrange("p a b -> p (a b)").bitcast(F32)
                K.dma("sp", fb[:, 0:2048], o_ap[tt * 128:(tt + 1) * 128, cs], reads=[Bo], writes=[Bw[bsl]], sembuf=Bfin[bsl])
                K.op("dve", lambda e: e.scalar_tensor_tensor(out=fb[:, 0:2048], in0=fb[:, 0:2048],
                                                             scalar=rs[:, tt:tt + 1], in1=gb[:, cs],
                                                             op0=ALU.mult, op1=ALU.mult),
                     reads=[Bw[bsl], Brs, Bgb], writes=[Bw[bsl]])
                K.dma("sp", o_ap[tt * 128:(tt + 1) * 128, cs], fb[:, 0:2048], reads=[Bw[bsl]], writes=[Bo2],
                      acc=True, sembuf=Bfin[bsl])
        K.retire(Bw + Bxr + Bpacc + [Bsqj, Bssq, Bgb, Brs])


LAMBDA_INIT0 = 0.8 - 0.6 * math.exp(-0.3 * 0)


def build_B_even():
    cx = Ctx()
    nc, K = cx.nc, cx.K
    qan = cx.din("qa_nope", [16, 128, NT], BF16)
    qar = cx.din("qa_rope", [8, 128, NT], BF16)
    ga = cx.din("ga", [16, 128, NT], F32)
    dq = cx.din("dq", [16, 128, NT], BF16)
    gb = cx.din("gb", [16, 128, NT], F32)
    kan = cx.din("kan_g", [4, 16, 128, NT], BF16)
    kar = cx.din("kar_g", [4, 128, NT], BF16)
    vag = cx.din("va_g", [4, 16, 128, 8, 128], BF16)
    dkg = cx.din("dk_g", [4, 16, 128, NT], BF16)
    dvg = cx.din("dv_g", [4, 16, 128, 8, 128], BF16)
    mask = cx.din("mask", [128, 4, 128], BF16)
    x = cx.din("x", [NT, D_MODEL], F32)
    wout = cx.din("wout", [16, 128, 32, 256], F32)
    lamb = cx.din("lamb", [1, 512], F32)
    sublng = cx.din("sublng", [128, 2], F32)
    x1, Bx1 = cx.dout("x1", [NT, D_MODEL], F32)

    with cx.es:
        c = make_consts(cx)
        mixT = cx.sb("mixT", [128, 32, NT], BF16)
        BmixT = K.buf("mixT")
        with ExitStack() as es:
            sub = cx.es
            cx.es = es
            A = attn_setup(cx, c, 2, mask)
            krT = cx.sb("krT", [128, 4, NT], BF16)
            BkrT = K.buf("krT")
            K.dma("sp", krT[:], kar.rearrange("j p t -> p j t"), writes=[BkrT])
            qrT = [cx.sb("qrT%d" % i, [128, NT], BF16) for i in range(2)]
            BqrT = K.bufs("qrT", 2)
            o0 = cx.sb("o0", [128, 2, NT], F32)
            Bo0 = K.buf("o0")
            comb = cx.sb("comb", [128, 2, 512], F32)
            Bcomb = K.buf("comb")
            sqb = cx.sb("sqb", [128, 512], BF16)
            Bsqb = K.buf("sqb")
            rsd = cx.sb("rsd", [128, 512], F32)
            Brsd = K.buf("rsd")
            lam_b = cx.sb("lam_b", [128, 4, 128], F32)
            lprod = cx.sb("lprod", [128, 2, 128], F32)
            lsum = cx.sb("lsum", [128, 2], F32)
            neglam = cx.sb("neglam", [128, 1], F32)
            sg = cx.sb("sg", [128, 2], F32)
            Blam = K.buf("lam")
            K.dma("sp", lam_b[:].rearrange("p a b -> p (a b)"), lamb.to_broadcast([128, 512]), writes=[Blam], acc=True)
            K.dma("sp", sg[:], sublng, writes=[Blam], acc=True)
            lv = lam_b[:].rearrange("p (a two) b -> p a two b", two=2)
            K.op("dve", lambda e: e.tensor_tensor(out=lprod[:], in0=lv[:, :, 0, :], in1=lv[:, :, 1, :], op=ALU.mult),
                 reads=[Blam], writes=[Blam])
            K.op("dve", lambda e: e.tensor_reduce(out=lsum[:], in_=lprod[:], axis=AX.X, op=ALU.add),
                 reads=[Blam], writes=[Blam])
            K.op("act", lambda e: e.activation(out=lsum[:], in_=lsum[:], func=AF.Exp), reads=[Blam], writes=[Blam])
            K.op("dve", lambda e: e.scalar_tensor_tensor(out=neglam[:], in0=lsum[:, 1:2], scalar=-LAMBDA_INIT0,
                                                         in1=lsum[:, 0:1], op0=ALU.add, op1=ALU.subtract),
                 reads=[Blam], writes=[Blam])
            K.op("dve", lambda e: e.tensor_scalar(out=sg[:], in0=sg[:], scalar1=1.0 - LAMBDA_INIT0, scalar2=None,
                                                  op0=ALU.mult), reads=[Blam], writes=[Blam])

            mla_scale = (128 + 64) ** -0.5
            for h in range(16):
                sl = h % 2
                load_head(cx, A, sl, kan[:, h], [vag[:, h]], qan[h], ga[h])
                if h % 2 == 0:
                    K.dma("sp", qrT[(h // 2) % 2][:], qar[h // 2], writes=[BqrT[(h // 2) % 2]])
                qr, Bqr = qrT[(h // 2) % 2], BqrT[(h // 2) % 2]
                r0 = 64 * (h % 2)
                parts = [(lambda jp, sp, sl=sl: A.kT[sl][:, jp, sp * 128:(sp + 1) * 128],
                          lambda lo, hi, sl=sl: A.qT[sl][:, lo:hi]),
                         (lambda jp, sp, r0=r0: krT[r0:r0 + 64, jp, sp * 128:(sp + 1) * 128],
                          lambda lo, hi, r0=r0, qr=qr: qr[r0:r0 + 64, lo:hi])]

                def fin(g, po, Bpo, pd, Bpd, h=h, sl=sl):
                    cs = slice(g * 512, (g + 1) * 512)
                    u = g % 2
                    K.op("dve", lambda e: e.reciprocal(out=A.rden[:], in_=pd[:]), reads=[Bpd], writes=[A.Brden])
                    K.op("dve", lambda e: e.tensor_tensor(out=A.osb[u][:], in0=po[0][:], in1=A.rden[:], op=ALU.mult),
                         reads=[Bpo[0], A.Brden], writes=[A.Bosb[u]])
                    K.op("pool", lambda e: e.tensor_tensor(out=mixT[:, h, cs], in0=A.osb[u][:], in1=A.gt[sl][:, cs],
                                                           op=ALU.mult),
                         reads=[A.Bosb[u], A.Bgt[sl]], writes=[BmixT], acc=True)
                attn_core(cx, c, A, parts, [A.BkT[sl], A.BqT[sl], BkrT, Bqr], A.vS[sl], A.BvS[sl], mla_scale, fin)

            diff_scale = 128 ** -0.5
            for h in range(8):
                for cc in range(2):
                    vh = h * 2 + cc
                    sl = vh % 2
                    vsl = h % 2
                    load_head(cx, A, sl, dkg[:, vh], None, dq[vh], None)
                    if cc == 0:
                        for ec in range(2):
                            K.dma("sp", A.vS[vsl][:, ec].rearrange("p (j s) d -> p j (s d)", j=4),
                                  dvg[:, h * 2 + ec].rearrange("j p s d -> p j (s d)"), writes=[A.BvS[vsl]],
                                  acc=(ec > 0))
                    parts = [(lambda jp, sp, sl=sl: A.kT[sl][:, jp, sp * 128:(sp + 1) * 128],
                              lambda lo, hi, sl=sl: A.qT[sl][:, lo:hi])]

                    def fin(g, po, Bpo, pd, Bpd, h=h, cc=cc):
                        cs = slice(g * 512, (g + 1) * 512)
                        K.op("dve", lambda e: e.reciprocal(out=A.rden[:], in_=pd[:]), reads=[Bpd], writes=[A.Brden])
                        if cc == 0:
                            for ec in range(2):
                                K.op("dve", lambda e: e.tensor_tensor(out=o0[:, ec, cs], in0=po[ec][:], in1=A.rden[:],
                                                                      op=ALU.mult),
                                     reads=[Bpo[ec], A.Brden], writes=[Bo0], acc=True)
                            return
                        for ec in range(2):
                            K.op("dve", lambda e: e.tensor_tensor(out=A.osb[ec][:], in0=po[ec][:], in1=A.rden[:],
                                                                  op=ALU.mult),
                                 reads=[Bpo[ec], A.Brden], writes=[A.Bosb[ec]])
                            K.op("dve", lambda e: e.scalar_tensor_tensor(out=comb[:, ec, :], in0=A.osb[ec][:],
                                                                         scalar=neglam[:, 0:1], in1=o0[:, ec, cs],
                                                                         op0=ALU.mult, op1=ALU.add),
                                 reads=[A.Bosb[ec], Blam, Bo0], writes=[Bcomb], acc=(ec > 0))
                        pst, Bpst = A.ps[0], A.Bps[0]
                        for ec in range(2):
                            K.op("act", lambda e: e.activation(out=sqb[:], in_=comb[:, ec, :], func=AF.Square),
                                 reads=[Bcomb], writes=[Bsqb])
                            K.op("pe", lambda e: e.matmul(pst[:], lhsT=c["oneb"][:], rhs=sqb[:], start=(ec == 0),
                                                          stop=(ec == 1)),
                                 reads=[Bsqb, c["B"]], writes=[Bpst])
                        K.op("act", lambda e: e.activation(out=rsd[:], in_=pst[:], func=AF.Sqrt, scale=1.0 / 256,
                                                           bias=c["eps"][:, 0:1]),
                             reads=[Bpst, c["B"]], writes=[Brsd])
                        K.op("dve", lambda e: e.reciprocal(out=rsd[:], in_=rsd[:]), reads=[Brsd], writes=[Brsd])
                        for ec in range(2):
                            gsl = (h * 2 + ec) % 2
                            K.dma("sp", A.gt[gsl][:, cs], gb[h * 2 + ec][:, cs], writes=[A.Bgt[gsl]])
                            K.op("dve", lambda e: e.tensor_tensor(out=comb[:, ec, :], in0=comb[:, ec, :], in1=rsd[:],
                                                                  op=ALU.mult),
                                 reads=[Bcomb, Brsd], writes=[Bcomb])
                            K.op("dve", lambda e: e.scalar_tensor_tensor(out=mixT[:, 16 + h * 2 + ec, cs],
                                                                         in0=comb[:, ec, :], scalar=sg[:, ec:ec + 1],
                                                                         in1=A.gt[gsl][:, cs], op0=ALU.mult,
                                                                         op1=ALU.mult),
                                 reads=[Bcomb, Blam, A.Bgt[gsl]], writes=[BmixT], acc=True)
                    attn_core(cx, c, A, parts, [A.BkT[sl], A.BqT[sl]], A.vS[vsl], A.BvS[vsl], diff_scale, fin, n_ech=2)
            allb = (A.BkT + A.BvS + A.BqT + A.Bgt + A.Bpt + [A.Brden] + A.Bosb + [A.Bmask] + A.Bps + A.Bpo[0] + A.Bpo[1]
                    + A.Bpd + [BkrT] + BqrT + [Bo0, Bcomb, Bsqb, Brsd, Blam])
            K.retire(allb)
            cx.es = sub
        out_proj(cx, c, mixT, BmixT, wout, x, x1, Bx1)
        K.finish(cx.outs)
    return cx


ODD_OFF = dict(dsa_q=0, dsa_k=2048, dsa_v=2176, idx_q=2304, idx_k=3328, idx_w=3392, gate_c=3408,
               fox_q=5456, fox_k=7504, fox_v=9552, fox_f=11600, gate_d=11616)
ODD_BLOCKS = 16 + 1 + 1 + 8 + 1 + 1 + 16 * 5
IDX_W_SCALE = (16 ** -0.5) * (64 ** -0.5)


def odd_col_lists():
    O = ODD_OFF
    L = []
    for b in range(16):
        L.append(rng_(O["dsa_q"] + b * 128, 128))
    L.append(rng_(O["dsa_k"], 128))
    L.append(rng_(O["dsa_v"], 128))
    for b in range(8):
        L.append(rng_(O["idx_q"] + b * 128, 128))
    L.append(rng_(O["idx_k"], 64) + rng_(O["idx_k"], 64))
    L.append(rng_(O["idx_w"], 16) + [-1] * 16 + rng_(O["fox_f"], 16) + [-1] * 80)
    for nm in ("gate_c", "fox_q", "fox_k", "fox_v", "gate_d"):
        for b in range(16):
            L.append(rng_(O[nm] + b * 128, 128))
    return L


def build_C_odd(limit=10**9):
    cx = Ctx()
    nc, K = cx.nc, cx.K
    x = cx.din("x", [NT, D_MODEL], F32)
    g0 = cx.din("g0", [128, 32], F32)
    win = cx.din("win", [ODD_BLOCKS, 128, 32, 128], F32)
    fbias = cx.din("fbias", [128, 1], F32)
    rB = [cx.din("ropeB_" + n, sh, F32) for n, sh in (("C", [128, NT]), ("S", [128, NT]), ("R", [128, 128]))]
    rI = [cx.din("ropeI_" + n, sh, F32) for n, sh in (("C", [128, NT]), ("S", [128, NT]), ("R", [128, 128]))]
    dsq, Bdsq = cx.dout("dsq", [16, 128, NT], BF16)
    dsk, Bdsk = cx.dout("dsk", [128, NT], BF16)
    dsv, Bdsv = cx.dout("dsv", [128, 8, 128], BF16)
    iq, Biq = cx.dout("iq", [8, 128, NT], BF16)
    ik, Bik = cx.dout("ik", [128, NT], BF16)
    iw, Biw = cx.dout("iw", [128, 8, 16], F32)
    logf, Blogf = cx.dout("logf", [16, NT], F32)
    gc, Bgc = cx.dout("gc", [16, 128, NT], F32)
    fq, Bfq = cx.dout("fq", [16, 128, NT], BF16)
    fk, Bfk = cx.dout("fk", [16, 128, NT], BF16)
    fv, Bfv = cx.dout("fv", [16, 128, 8, 128], BF16)
    gd, Bgd = cx.dout("gd", [16, 128, NT], F32)

    with cx.es:
        c = make_consts(cx)
        gs = cx.sb("gs", [128, 32], F32)
        fbs = cx.sb("fbs", [128, 1], F32)
        Bgs = K.buf("gs")
        K.dma("sp", gs[:], g0, writes=[Bgs], acc=True)
        K.dma("sp", fbs[:], fbias, writes=[Bgs], acc=True)
        hT = cx.sb("hT", [128, 32, NT], BF16)
        BhT = K.buf("hT")
        norm_transpose(cx, c, x, gs, Bgs, hT, BhT)
        P = proj_setup(cx, c, 2)
        P.hT, P.BhT = hT, BhT
        P.limit = limit
        load_rope(cx, P, 0, *rB)
        load_rope(cx, P, 1, *rI)
        wmid = cx.sb("wmid", [16, NT], BF16)
        wlo = cx.sb("wlo", [16, NT], BF16)
        whi = cx.sb("whi", [16, NT], BF16)
        wf = cx.sb("wf", [128, NT], F32)
        iws = cx.sb("iws", [128, 8, 16], F32)
        Bwsp, Bwf, Biws = K.buf("wsplit"), K.buf("wf"), K.buf("iws")

        def post_misc(bi, t, ps, Bps):
            if t is not None:
                cs = slice(t * 512, (t + 1) * 512)
                K.op("dve", lambda e: e.tensor_copy(out=wf[0:64, cs], in_=ps[0:64, :]), reads=[Bps], writes=[Bwf],
                     acc=(t > 0))
                return
            K.op("act", lambda e: e.activation(out=wf[32:48, :], in_=wf[32:48, :], func=AF.Exp, scale=-1.0,
                                               bias=fbs[32:48, 0:1]), reads=[Bwf, Bgs], writes=[Bwf])
            K.op("act", lambda e: e.activation(out=wf[32:48, :], in_=wf[32:48, :], func=AF.Ln, scale=1.0,
                                               bias=c["one1"][32:48, 0:1]), reads=[Bwf, c["B"]], writes=[Bwf])
            K.op("act", lambda e: e.activation(out=wf[32:48, :], in_=wf[32:48, :], func=AF.Copy, scale=-1.0),
                 reads=[Bwf], writes=[Bwf])
            K.dma("sp", logf, wf[32:48, :], reads=[Bwf], writes=[Blogf], store=True, sembuf=Bwf)
            K.op("dve", lambda e: e.tensor_copy(out=whi[:], in_=wf[0:16, :]), reads=[Bwf], writes=[Bwsp])
            K.op("dve", lambda e: e.tensor_tensor(out=wf[0:16, :], in0=wf[0:16, :], in1=whi[:], op=ALU.subtract),
                 reads=[Bwf, Bwsp], writes=[Bwf])
            K.op("dve", lambda e: e.tensor_copy(out=wmid[:], in_=wf[0:16, :]), reads=[Bwf], writes=[Bwsp], acc=True)
            K.op("dve", lambda e: e.tensor_tensor(out=wf[0:16, :], in0=wf[0:16, :], in1=wmid[:], op=ALU.subtract),
                 reads=[Bwf, Bwsp], writes=[Bwf])
            K.op("dve", lambda e: e.tensor_copy(out=wlo[:], in_=wf[0:16, :]), reads=[Bwf], writes=[Bwsp], acc=True)
            pt_ = P.paux[0]
            for s in range(8):
                for pi, part in enumerate((whi, wmid, wlo)):
                    K.op("pe", lambda e: e.matmul(pt_[:, s * 16:(s + 1) * 16], lhsT=part[:, s * 128:(s + 1) * 128],
                                                  rhs=c["idb"][0:16, 0:16], start=(pi == 0), stop=(pi == 2)),
                         reads=[Bwsp, c["B"]], writes=[P.Bpaux[0]])
            K.op("act", lambda e: e.activation(out=iws[:].rearrange("p s h -> p (s h)"), in_=pt_[:, 0:128],
                                               func=AF.Copy, scale=IDX_W_SCALE), reads=[P.Bpaux[0]], writes=[Biws])
            K.dma("sp", iw, iws[:], reads=[Biws], writes=[Biw], store=True)

        bi = 0
        for b in range(16):
            proj_block(cx, c, P, win[bi], 32, P.hT, P.BhT, post_rope(cx, c, P, 0, dsq[b], Bdsq)); bi += 1
        proj_block(cx, c, P, win[bi], 32, P.hT, P.BhT, post_rope(cx, c, P, 0, dsk, Bdsk)); bi += 1
        proj_block(cx, c, P, win[bi], 32, P.hT, P.BhT, post_transpose(cx, c, P, dsv, Bdsv)); bi += 1
        for b in range(8):
            proj_block(cx, c, P, win[bi], 32, P.hT, P.BhT, post_rope(cx, c, P, 1, iq[b], Biq)); bi += 1
        proj_block(cx, c, P, win[bi], 32, P.hT, P.BhT, post_rope(cx, c, P, 1, ik, Bik)); bi += 1
        proj_block(cx, c, P, win[bi], 32, P.hT, P.BhT, post_misc); bi += 1
        for b in range(16):
            proj_block(cx, c, P, win[bi], 32, P.hT, P.BhT, post_store_f32(cx, P, gc[b], Bgc, AF.Silu)); bi += 1
        for b in range(16):
            proj_block(cx, c, P, win[bi], 32, P.hT, P.BhT, post_store_bf16(cx, P, fq[b], Bfq)); bi += 1
        for b in range(16):
            proj_block(cx, c, P, win[bi], 32, P.hT, P.BhT, post_store_bf16(cx, P, fk[b], Bfk)); bi += 1
        for b in range(16):
            proj_block(cx, c, P, win[bi], 32, P.hT, P.BhT, post_transpose(cx, c, P, fv[b], Bfv)); bi += 1
        for b in range(16):
            proj_block(cx, c, P, win[bi], 32, P.hT, P.BhT, post_store_f32(cx, P, gd[b], Bgd, AF.Silu)); bi += 1
        K.finish(cx.outs)
    return cx


def host_C_odd(inp, xs, cores):
    win = tile_w(inp["odd_w_in"][0], odd_col_lists())
    g0 = np.ascontiguousarray(inp["odd_norm"][0].reshape(32, 128).T)
    fb = np.zeros((128, 1), np.float32)
    fb[32:48, 0] = -inp["fox_forget_bias"][0]
    maps = []
    for i, r in enumerate(cores):
        b, j = divmod(r, 4)
        pos = core_positions(j)
        CB, SB, RB = rope_table(pos, 128, 32)
        CI, SI, RI = rope_table(pos, 64, 16)
        maps.append({"x": xs[i], "g0": g0, "win": win, "fbias": fb, "ropeB_C": CB, "ropeB_S": SB, "ropeB_R": RB,
                     "ropeI_C": CI, "ropeI_S": SI, "ropeI_R": RI})
    return maps


SLOT_OFF = [sum(4 * t + 4 for t in range(s)) for s in range(9)]


def host_negmask(j):
    m = np.full((128, 4, 128), NEG, np.float32)
    q = np.arange(128)[:, None]
    k = np.arange(128)[None, :]
    for jp in range(4):
        if jp < j:
            m[:, jp, :] = 0.0
        elif jp == j:
            m[:, jp, :] = np.where(k <= q, 0.0, NEG)
    return m


def build_D_odd(dsa_slots=8, fox_heads=16):
    cx = Ctx()
    nc, K = cx.nc, cx.K
    dsq = cx.din("dsq", [16, 128, NT], BF16)
    iq = cx.din("iq", [8, 128, NT], BF16)
    iw = cx.din("iw", [128, 8, 16], F32)
    gc = cx.din("gc", [16, 128, NT], F32)
    fq = cx.din("fq", [16, 128, NT], BF16)
    gd = cx.din("gd", [16, 128, NT], F32)
    dskg = cx.din("dsk_g", [4, 128, NT], BF16)
    dsvg = cx.din("dsv_g", [4, 128, 8, 128], BF16)
    ikg = cx.din("ik_g", [4, 128, NT], BF16)
    fkg = cx.din("fk_g", [4, 16, 128, NT], BF16)
    fvg = cx.din("fv_g", [4, 16, 128, 8, 128], BF16)
    logfg = cx.din("logf_g", [4, 16, NT], F32)
    mask = cx.din("mask", [128, 4, 128], BF16)
    negm_d = cx.din("negmask", [128, 4, 128], F32)
    sel_d = cx.din("sel", [128, 4], F32)
    x = cx.din("x", [NT, D_MODEL], F32)
    wout = cx.din("wout", [16, 128, 32, 256], F32)
    fing = cx.din("fing", [1, D_MODEL], F32)
    out, Bout = cx.dout("out", [NT, D_MODEL], F32)
    cref_d = nc.dram_tensor("cref_d", [16, 8], F32, kind="Internal").ap()
    Bcref_d = K.buf("cref_d")

    with cx.es:
        c = make_consts(cx)
        mixT = cx.sb("mixT", [128, 32, NT], BF16)
        BmixT = K.buf("mixT")
        biasT = cx.sb("biasT", [128, 144, 16], F32)
        BbiasT = K.buf("biasT")
        with ExitStack() as es:
            lf = cx.sb("lf", [16, SEQ], F32, es)
            one16 = cx.sb("one16", [16, SEQ], F32, es)
            cum = cx.sb("cum", [16, SEQ], F32, es)
            parts = [cx.sb("cpart%d" % i, [16, SEQ], BF16, es) for i in range(3)]
            sel = cx.sb("sel_sb", [128, 4], F32, es)
            crefT = cx.sb("crefT", [16, 8], F32, es)
            crefb = cx.sb("crefb", [128, 16, 8], F32, es)
            ckT = cx.sb("ckT", [128, 32, 16], F32, es)
            pck = cx.ps("pck", [128, 512], F32, es)
            Blf, Bone, Bcum, Bparts, Bsel, BcrefT, Bcrefb, BckT, Bpck = (K.buf(n) for n in
                                                                          "lf one16 cum cparts sel crefT crefb ckT pck".split())
            lfv = lf[:].rearrange("h (s j i) -> h s j i", s=8, j=4)
            for jp in range(4):
                K.dma("sp", lfv[:, :, jp, :], logfg[jp].rearrange("h (s i) -> h s i", i=128), writes=[Blf], acc=(jp > 0))
            K.dma("sp", sel[:], sel_d, writes=[Bsel])
            K.op("pool", lambda e: e.memset(one16[:], 1.0), writes=[Bone])
            K.op("dve", lambda e: e.tensor_tensor_scan(out=cum[:], data0=one16[:], data1=lf[:], initial=0.0,
                                                       op0=ALU.mult, op1=ALU.add),
                 reads=[Bone, Blf], writes=[Bcum])
            cumv = cum[:].rearrange("h (s j i) -> h s j i", s=8, j=4)
            K.op("dve", lambda e: e.tensor_scalar(out=crefT[:], in0=cumv[:, :, 0, 127], scalar1=sel[0:16, 0:1],
                                                  scalar2=None, op0=ALU.mult), reads=[Bcum, Bsel], writes=[BcrefT])
            for jp in range(1, 4):
                K.op("dve", lambda e: e.scalar_tensor_tensor(out=crefT[:], in0=cumv[:, :, jp, 127],
                                                             scalar=sel[0:16, jp:jp + 1], in1=crefT[:],
                                                             op0=ALU.mult, op1=ALU.add),
                     reads=[Bcum, Bsel, BcrefT], writes=[BcrefT])
            K.dma("sp", cref_d, crefT[:], reads=[BcrefT], writes=[Bcref_d], store=True)
            K.dma("sp", crefb[:].rearrange("p h s -> p (h s)"),
                  cref_d.rearrange("h s -> (h s)").unsqueeze(0).to_broadcast([128, 128]),
                  reads=[Bcref_d], writes=[Bcrefb])
            res = one16
            K.op("dve", lambda e: e.tensor_copy(out=parts[0][:], in_=cum[:]), reads=[Bcum], writes=[Bparts])
            K.op("dve", lambda e: e.tensor_tensor(out=res[:], in0=cum[:], in1=parts[0][:], op=ALU.subtract),
                 reads=[Bcum, Bparts], writes=[Bone])
            K.op("dve", lambda e: e.tensor_copy(out=parts[1][:], in_=res[:]), reads=[Bone], writes=[Bparts], acc=True)
            K.op("dve", lambda e: e.tensor_tensor(out=res[:], in0=res[:], in1=parts[1][:], op=ALU.subtract),
                 reads=[Bone, Bparts], writes=[Bone])
            K.op("dve", lambda e: e.tensor_copy(out=parts[2][:], in_=res[:]), reads=[Bone], writes=[Bparts], acc=True)
            for kc in range(32):
                for pi in range(3):
                    K.op("pe", lambda e: e.matmul(pck[:, kc * 16:(kc + 1) * 16],
                                                  lhsT=parts[pi][:, kc * 128:(kc + 1) * 128],
                                                  rhs=c["idb"][0:16, 0:16], start=(pi == 0), stop=(pi == 2)),
                         reads=[Bparts, c["B"]], writes=[Bpck])
            K.op("act", lambda e: e.activation(out=ckT[:].rearrange("p c h -> p (c h)"), in_=pck[:], func=AF.Copy),
                 reads=[Bpck], writes=[BckT])
            for s in range(8):
                n_s = 4 * s + 4
                o_s = SLOT_OFF[s]
                K.op("dve", lambda e: e.tensor_tensor(out=biasT[:, o_s:o_s + n_s, :],
                                                      in0=crefb[:, :, s].unsqueeze(1).to_broadcast([128, n_s, 16]),
                                                      in1=ckT[:, 0:n_s, :], op=ALU.subtract),
                     reads=[Bcrefb, BckT], writes=[BbiasT], acc=(s > 0))
            K.op("dve", lambda e: e.tensor_scalar(out=biasT[:], in0=biasT[:], scalar1=0.0, scalar2=None, op0=ALU.min),
                 reads=[BbiasT], writes=[BbiasT])
            K.retire([Blf, Bone, Bcum, Bparts, Bsel, BcrefT, Bcrefb, BckT, Bpck])

        dsa_scale = 128 ** -0.5
        with ExitStack() as es:
            ikT = cx.sb("ikT", [128, 4, NT], BF16, es)
            dskT = cx.sb("dskT", [128, 4, NT], BF16, es)
            dsvS = cx.sb("dsvS", [128, 32, 128], BF16, es)
            iqs2 = [cx.sb("iqS%d" % i, [128, 8, 128], BF16, es) for i in range(2)]
            dsqs2 = [cx.sb("dsqS%d" % i, [128, 16, 128], BF16, es) for i in range(2)]
            Bqs = K.bufs("dsa_q", 2)
            iws = cx.sb("iws", [128, 8, 16], F32, es)
            negm = cx.sb("negm", [128, 4, 128], F32, es)
            score = cx.sb("score", [128, SEQ], F32, es)
            work = cx.sb("work", [128, SEQ], F32, es)
            Rr = [cx.sb("Rr%d" % i, [128, 512], F32, es) for i in range(2)]
            m8 = cx.sb("m8", [128, 8], F32, es)
            maskq = cx.sb("maskq", [128, SEQ], BF16, es)
            maskT = cx.sb("maskT", [128, 32, 128], BF16, es)
            pt = [cx.sb("dpt%d" % i, [128, 512], BF16, es) for i in range(3)]
            gt4 = [cx.sb("gt4%d" % i, [128, 4, 128], F32, es) for i in range(2)]
            rden = cx.sb("drden", [128, 512], F32, es)
            osb = cx.sb("dosb", [128, 512], F32, es)
            ps = [cx.ps("dps%d" % i, [128, 512], F32, es) for i in range(2)]
            po = [cx.ps("dpo%d" % i, [128, 512], F32, es) for i in range(2)]
            pd = [cx.ps("dpd%d" % i, [128, 512], F32, es) for i in range(2)]
            ptr = cx.ps("dptr", [128, 1024], BF16, es)
            Bin = K.buf("dsa_in")
            Bscore, Bwork, Bm8, Bmaskq, BmaskT, Brden, Bosb, Bptr = (K.buf(n) for n in
                                                                     "score work m8 maskq maskT drden dosb dptr".split())
            BRr, Bpt, Bgt4, Bps, Bpo, Bpd = K.bufs("Rr", 2), K.bufs("dpt", 3), K.bufs("gt4", 2), K.bufs("dps", 2), \
                K.bufs("dpo", 2), K.bufs("dpd", 2)
            K.dma("sp", ikT[:], ikg.rearrange("j p t -> p j t"), writes=[Bin], acc=True)
            K.dma("sp", dskT[:], dskg.rearrange("j p t -> p j t"), writes=[Bin], acc=True)
            K.dma("sp", dsvS[:].rearrange("p (j s) d -> p j (s d)", j=4), dsvg.rearrange("j p s d -> p j (s d)"),
                  writes=[Bin], acc=True)
            K.dma("sp", iws[:], iw, writes=[Bin], acc=True)
            K.dma("sp", negm[:], negm_d, writes=[Bin], acc=True)
            it = 0
            grp = 0
            for s in range(dsa_slots):
                qs = slice(s * 128, (s + 1) * 128)
                iqS, dsqS, Bq = iqs2[s % 2], dsqs2[s % 2], Bqs[s % 2]
                K.dma("sp", iqS[:], iq[:, :, qs].rearrange("b p t -> p b t"), writes=[Bq])
                K.dma("sp", dsqS[:], dsq[:, :, qs].rearrange("b p t -> p b t"), writes=[Bq], acc=True)
                for sp in range(s + 1):
                    ks = slice(sp * 512, (sp + 1) * 512)
                    for h in range(16):
                        r0 = 64 * (h % 2)
                        u = it % 2
                        it += 1
                        K.op("pe", lambda e: e.matmul(ps[u][:], lhsT=iqS[r0:r0 + 64, h // 2, :],
                                                      rhs=ikT[r0:r0 + 64, :, sp * 128:(sp + 1) * 128],
                                                      start=True, stop=True), reads=[Bin, Bq], writes=[Bps[u]])
                        K.op("act", lambda e: e.activation(out=Rr[u][:], in_=ps[u][:], func=AF.Relu),
                             reads=[Bps[u]], writes=[BRr[u]])
                        if h == 0:
                            K.op("dve", lambda e: e.tensor_scalar(out=score[:, ks], in0=Rr[u][:],
                                                                  scalar1=iws[:, s, 0:1], scalar2=None, op0=ALU.mult),
                                 reads=[BRr[u], Bin], writes=[Bscore], acc=(sp > 0))
                        else:
                            K.op("dve", lambda e: e.scalar_tensor_tensor(out=score[:, ks], in0=Rr[u][:],
                                                                         scalar=iws[:, s, h:h + 1], in1=score[:, ks],
                                                                         op0=ALU.mult, op1=ALU.add),
                                 reads=[BRr[u], Bin, Bscore], writes=[Bscore])
                    if sp == s:
                        K.op("dve", lambda e: e.tensor_tensor(out=score[:, ks], in0=score[:, ks],
                                                              in1=negm[:].rearrange("p j k -> p (j k)"), op=ALU.add),
                             reads=[Bscore, Bin], writes=[Bscore])
                N = (s + 1) * 512
                for rd in range(32):
                    src = score if rd == 0 else work
                    Bsrc = Bscore if rd == 0 else Bwork
                    K.op("dve", lambda e: e.max(out=m8[:], in_=src[:, 0:N]), reads=[Bsrc], writes=[Bm8])
                    if rd < 31:
                        K.op("dve", lambda e: e.match_replace(out=work[:, 0:N], in_to_replace=m8[:],
                                                              in_values=src[:, 0:N], imm_value=NEG),
                             reads=[Bm8, Bsrc], writes=[Bwork])
                K.op("dve", lambda e: e.tensor_scalar(out=m8[:, 7:8], in0=m8[:, 7:8], scalar1=-1.0e29, scalar2=None,
                                                      op0=ALU.max), reads=[Bm8], writes=[Bm8])
                K.op("dve", lambda e: e.tensor_scalar(out=maskq[:, 0:N], in0=score[:, 0:N], scalar1=m8[:, 7:8],
                                                      scalar2=None, op0=ALU.is_ge),
                     reads=[Bscore, Bm8], writes=[Bmaskq])
                nch = 4 * s + 4
                for c8 in range(0, nch, 8):
                    nn = min(8, nch - c8)
                    for ci in range(nn):
                        K.op("pe", lambda e: e.transpose(out=ptr[:, ci * 128:(ci + 1) * 128],
                                                         in_=maskq[:, (c8 + ci) * 128:(c8 + ci + 1) * 128],
                                                         identity=c["idb"][:]),
                             reads=[Bmaskq, c["B"]], writes=[Bptr], acc=(ci > 0))
                    K.op("act", lambda e: e.activation(out=maskT[:, c8:c8 + nn, :].rearrange("p c q -> p (c q)"),
                                                       in_=ptr[:, 0:nn * 128], func=AF.Copy),
                         reads=[Bptr], writes=[BmaskT], acc=(c8 > 0))
                for hg in range(4):
                    pb = grp % 2
                    grp += 1
                    gu = grp % 2
                    for hh in range(4):
                        K.dma("sp", gt4[gu][:, hh, :], gc[4 * hg + hh][:, qs], writes=[Bgt4[gu]], acc=(hh > 0))
                    for kc in range(nch):
                        jp, sp = kc % 4, kc // 4
                        u = it % 2
                        v = it % 3
                        it += 1
                        K.op("pe", lambda e: e.matmul(ps[u][:], lhsT=dskT[:, jp, sp * 128:(sp + 1) * 128],
                                                      rhs=dsqS[:, 4 * hg:4 * hg + 4, :], start=True, stop=True),
                             reads=[Bin, Bq], writes=[Bps[u]])
                        K.op("act", lambda e: e.activation(out=pt[v][:], in_=ps[u][:], func=AF.Exp, scale=dsa_scale),
                             reads=[Bps[u]], writes=[Bpt[v]])
                        ptv = pt[v][:].rearrange("p (h q) -> p h q", h=4)
                        K.op("pool", lambda e: e.tensor_tensor(out=ptv, in0=ptv,
                                                               in1=maskT[:, kc, :].unsqueeze(1).to_broadcast([128, 4, 128]),
                                                               op=ALU.mult),
                             reads=[Bpt[v], BmaskT], writes=[Bpt[v]])
                        K.op("pe", lambda e: e.matmul(po[pb][:], lhsT=dsvS[:, jp * 8 + sp, :], rhs=pt[v][:],
                                                      start=(kc == 0), stop=(kc == nch - 1)),
                             reads=[Bin, Bpt[v]], writes=[Bpo[pb]])
                        K.op("pe", lambda e: e.matmul(pd[pb][:], lhsT=c["oneb"][:], rhs=pt[v][:],
                                                      start=(kc == 0), stop=(kc == nch - 1)),
                             reads=[Bpt[v], c["B"]], writes=[Bpd[pb]])
                    K.op("dve", lambda e: e.reciprocal(out=rden[:], in_=pd[pb][:]), reads=[Bpd[pb]], writes=[Brden])
                    K.op("dve", lambda e: e.tensor_tensor(out=osb[:], in0=po[pb][:], in1=rden[:], op=ALU.mult),
                         reads=[Bpo[pb], Brden], writes=[Bosb])
                    K.op("dve", lambda e: e.tensor_tensor(out=mixT[:, 4 * hg:4 * hg + 4, qs],
                                                          in0=osb[:].rearrange("p (h q) -> p h q", h=4),
                                                          in1=gt4[gu][:], op=ALU.mult),
                         reads=[Bosb, Bgt4[gu]], writes=[BmixT], acc=True)
            K.retire([Bin, Bscore, Bwork, Bm8, Bmaskq, BmaskT, Brden, Bosb, Bptr] + Bqs + BRr + Bpt + Bgt4 + Bps + Bpo + Bpd)

        fox_scale = 128 ** -0.5
        with ExitStack() as es:
            sub = cx.es
            cx.es = es
            A = attn_setup(cx, c, 1, mask)
            for h in range(fox_heads):
                sl = h % 2
                load_head(cx, A, sl, fkg[:, h], [fvg[:, h]], fq[h], gd[h])
                parts = [(lambda jp, sp, sl=sl: A.kT[sl][:, jp, sp * 128:(sp + 1) * 128],
                          lambda lo, hi, sl=sl: A.qT[sl][:, lo:hi])]

                def exp_fn(g, kc, c0, psb, Bpsb, pt, Bpt, h=h):
                    s_lo = 4 * g + c0 // 128
                    for s in range(s_lo, 4 * g + 4):
                        cs = slice((s - 4 * g) * 128, (s - 4 * g + 1) * 128)
                        K.op("act", lambda e: e.activation(out=pt[:, cs], in_=psb[:, cs], func=AF.Exp,
                                                           scale=fox_scale,
                                                           bias=biasT[:, SLOT_OFF[s] + kc, h:h + 1]),
                             reads=[Bpsb, BbiasT], writes=[Bpt], acc=(s > s_lo))

                def fin(g, po, Bpo, pd, Bpd, h=h, sl=sl):
                    cs = slice(g * 512, (g + 1) * 512)
                    u = g % 2
                    K.op("dve", lambda e: e.reciprocal(out=A.rden[:], in_=pd[:]), reads=[Bpd], writes=[A.Brden])
                    K.op("dve", lambda e: e.tensor_tensor(out=A.osb[u][:], in0=po[0][:], in1=A.rden[:], op=ALU.mult),
                         reads=[Bpo[0], A.Brden], writes=[A.Bosb[u]])
                    K.op("pool", lambda e: e.tensor_tensor(out=mixT[:, 16 + h, cs], in0=A.osb[u][:],
                                                           in1=A.gt[sl][:, cs], op=ALU.mult),
                         reads=[A.Bosb[u], A.Bgt[sl]], writes=[BmixT], acc=True)
                attn_core(cx, c, A, parts, [A.BkT[sl], A.BqT[sl]], A.vS[sl], A.BvS[sl], fox_scale, fin, exp_fn=exp_fn)
            K.retire(A.BkT + A.BvS + A.BqT + A.Bgt + A.Bpt + [A.Brden] + A.Bosb + [A.Bmask] + A.Bps + A.Bpo[0]
                     + A.Bpo[1] + A.Bpd)
            cx.es = sub
        out_proj(cx, c, mixT, BmixT, wout, x, out, Bout, final_g=fing)
        K.finish(cx.outs)
    return cx


_CACHE = {}


def _prog(name, fn):
    if name not in _CACHE:
        _CACHE[name] = fn()
    return _CACHE[name]


def _gather(res, b, key):
    return np.ascontiguousarray(np.stack([res[4 * b + jp][key] for jp in range(4)]))


def kernel(**inp):
    inp = {k_: np.asarray(v) for k_, v in inp.items()}
    cores = list(range(8))
    cxA = _prog("A", build_A_even)
    resA = run_bass_kernel_spmd(cxA.nc, host_A_even(inp, cores), core_ids=cores).results
    cxB = _prog("B", build_B_even)
    woutE = tile_w_wide(inp["even_w_out"][0], 256)
    lamb = np.ascontiguousarray(inp["diff_lambda"][0].reshape(1, 512))
    sublng = np.ascontiguousarray(inp["diff_subln"][0].reshape(2, 128).T)
    gathered = {}
    for b in range(2):
        gathered[b] = {"kan_g": _gather(resA, b, "ka_nope"), "kar_g": _gather(resA, b, "ka_rope"),
                       "va_g": _gather(resA, b, "va"), "dk_g": _gather(resA, b, "dk"), "dv_g": _gather(resA, b, "dv")}
    mapsB = []
    for r in cores:
        b, j = divmod(r, 4)
        m = {"qa_nope": resA[r]["qa_nope"], "qa_rope": resA[r]["qa_rope"], "ga": resA[r]["ga"], "dq": resA[r]["dq"],
             "gb": resA[r]["gb"], "mask": diag_masks(j), "x": np.ascontiguousarray(inp["x"][b][core_positions(j)]),
             "wout": woutE, "lamb": lamb, "sublng": sublng}
        m.update(gathered[b])
        mapsB.append(m)
    resB = run_bass_kernel_spmd(cxB.nc, mapsB, core_ids=cores).results
    del resA, mapsB, gathered
    cxC = _prog("C", build_C_odd)
    x1 = [resB[r]["x1"] for r in cores]
    resC = run_bass_kernel_spmd(cxC.nc, host_C_odd(inp, x1, cores), core_ids=cores).results
    cxD = _prog("D", build_D_odd)
    woutO = tile_w_wide(inp["odd_w_out"][0], 256)
    fing = np.ascontiguousarray(inp["final_norm"].reshape(1, D_MODEL))
    gathered = {}
    for b in range(2):
        gathered[b] = {"dsk_g": _gather(resC, b, "dsk"), "dsv_g": _gather(resC, b, "dsv"), "ik_g": _gather(resC, b, "ik"),
                       "fk_g": _gather(resC, b, "fk"), "fv_g": _gather(resC, b, "fv"), "logf_g": _gather(resC, b, "logf")}
    mapsD = []
    for r in cores:
        b, j = divmod(r, 4)
        sel = np.zeros((128, 4), np.float32)
        sel[:, j] = 1.0
        m = {"dsq": resC[r]["dsq"], "iq": resC[r]["iq"], "iw": resC[r]["iw"], "gc": resC[r]["gc"], "fq": resC[r]["fq"],
             "gd": resC[r]["gd"], "mask": diag_masks(j), "negmask": host_negmask(j), "sel": sel, "x": x1[r],
             "wout": woutO, "fing": fing}
        m.update(gathered[b])
        mapsD.append(m)
    resD = run_bass_kernel_spmd(cxD.nc, mapsD, core_ids=cores).results
    out = np.zeros((2, SEQ, D_MODEL), np.float32)
    for r in cores:
        b, j = divmod(r, 4)
        out[b][core_positions(j)] = resD[r]["out"]
    return out
```
